# Optimizing a Trainium2 kernel written in Bass

```python
import math, functools
import jax, jax.numpy as jnp
from jax import lax
import numpy as np

D_MODEL = 2048
BATCH = 8
SEQ = 2048
DEPTH = 2
DEC_BATCH = 8
DEC_SEQ = 64
PAST_LEN = 2048

CHUNK = 64
D_FF = 3 * D_MODEL
D_RNN = D_MODEL
N_RG_HEADS = 16
RG_BLOCK = D_RNN // N_RG_HEADS
CONV_W = 4
LRU_C = 8.0
D_POOL = D_MODEL // 2
POOL_WINDOWS = (2, 4, 8, 16)
N_POOL_GROUPS = 4
POOL_GROUP = D_POOL // N_POOL_GROUPS
POOL_MAX = 16
IN_COLS = 2 * D_RNN + D_POOL + 2 * D_MODEL
EPS = 1e-6

kernel_name = "hawk_pool_macaron_stream_step"


def rms_norm(x, g):
    xf = x.astype(jnp.float32)
    y = xf * lax.rsqrt(jnp.mean(xf * xf, axis=-1, keepdims=True) + EPS)
    return (y * g.astype(jnp.float32)).astype(x.dtype)


def swiglu(x, w_in, w_out):
    gate, up = jnp.split(x @ w_in, 2, axis=-1)
    return (jax.nn.silu(gate) * up) @ w_out


def causal_conv(x, prefix, w, b):
    T = x.shape[1]
    z = jnp.concatenate([prefix.astype(x.dtype), x], axis=1)
    y = sum(z[:, k:k + T] * w[k] for k in range(CONV_W)) + b
    return y, z[:, -(CONV_W - 1):]


def rg_lru(x, h0, w_a, b_a, w_x, b_x, lam):
    B, T, C = x.shape
    xh = x.reshape(B, T, N_RG_HEADS, RG_BLOCK)
    r = jax.nn.sigmoid((jnp.einsum('btnc,ncd->btnd', xh, w_a).reshape(B, T, C) + b_a).astype(jnp.float32))
    i = jax.nn.sigmoid((jnp.einsum('btnc,ncd->btnd', xh, w_x).reshape(B, T, C) + b_x).astype(jnp.float32))
    log_a = -LRU_C * r * jax.nn.softplus(-lam.astype(jnp.float32))
    a = jnp.exp(log_a)
    u = jnp.sqrt(-jnp.expm1(2.0 * log_a)) * (i * x.astype(jnp.float32))

    def step(h, au):
        a_t, u_t = au
        h = a_t * h + u_t
        return h, h

    h_last, hs = lax.scan(step, h0.astype(jnp.float32), (jnp.swapaxes(a, 0, 1), jnp.swapaxes(u, 0, 1)))
    return jnp.swapaxes(hs, 0, 1).astype(x.dtype), h_last


def multiscale_pool(x, prefix, n_valid_prefix):
    B, T, _ = x.shape
    P = POOL_MAX - 1
    z = jnp.concatenate([prefix.astype(x.dtype), x], axis=1)
    zf = z.astype(jnp.float32)
    c = jnp.concatenate([jnp.zeros((B, 1, D_POOL), jnp.float32), jnp.cumsum(zf, axis=1)], axis=1)
    end = c[:, P + 1:]
    pos = jnp.arange(T)
    outs = []
    for g, w in enumerate(POOL_WINDOWS):
        sl = slice(g * POOL_GROUP, (g + 1) * POOL_GROUP)
        start = c[:, P + 1 - w:P + 1 - w + T, sl]
        cnt = jnp.minimum(pos + 1 + n_valid_prefix, w).astype(jnp.float32)[None, :, None]
        outs.append((end[..., sl] - start) / cnt)
    mean = jnp.concatenate(outs, axis=-1)
    return (mean - zf[:, P:]).astype(x.dtype), z[:, -P:]


def layer(x, conv_prefix, h0, pool_prefix, n_valid_pool, gains, w_ffn_in, w_ffn_out, w_in, conv_w, conv_b,
          w_rg_a, b_rg_a, w_rg_x, b_rg_x, lru_param, w_pool_mix, pool_scale, w_br_rg, w_br_pool, w_out):
    B, T, _ = x.shape
    x = x + 0.5 * rms_norm(swiglu(rms_norm(x, gains[0]), w_ffn_in[0], w_ffn_out[0]), gains[1])
    h = rms_norm(x, gains[2])
    proj = h @ w_in
    o1 = D_RNN
    o2 = o1 + D_RNN
    o3 = o2 + D_POOL
    o4 = o3 + D_MODEL
    x_rg, y_gelu, x_pool, g_rg, g_pool = proj[..., :o1], proj[..., o1:o2], proj[..., o2:o3], proj[..., o3:o4], proj[..., o4:]
    xc, conv_state = causal_conv(x_rg, conv_prefix, conv_w, conv_b)
    rec, h_last = rg_lru(xc, h0, w_rg_a, b_rg_a, w_rg_x, b_rg_x, lru_param)
    branch_rg = (rec * jax.nn.gelu(y_gelu)) @ w_br_rg
    pooled, pool_state = multiscale_pool(x_pool, pool_prefix, n_valid_pool)
    pooled = jnp.einsum('btgc,gcd->btgd', pooled.reshape(B, T, N_POOL_GROUPS, POOL_GROUP), w_pool_mix).reshape(B, T, D_POOL)
    branch_pool = (pooled * pool_scale) @ w_br_pool
    mix = (jax.nn.sigmoid(g_rg) * branch_rg + jax.nn.sigmoid(g_pool) * branch_pool) @ w_out
    x = x + rms_norm(mix, gains[3])
    x = x + 0.5 * rms_norm(swiglu(rms_norm(x, gains[4]), w_ffn_in[1], w_ffn_out[1]), gains[5])
    return x, conv_state, h_last, pool_state


def setup_inputs(seed: int = 0) -> dict:
    key = jax.random.key(seed)
    ks = jax.random.split(key, 24)

    def nrm(k, shape, scale):
        return jax.random.normal(k, shape, jnp.float32) * scale

    u = jax.random.uniform(ks[12], (DEPTH, D_RNN), jnp.float32, minval=0.9, maxval=0.999)
    return {
        "x_prompt": nrm(ks[0], (BATCH, SEQ, D_MODEL), 1.0),
        "x_sample": nrm(ks[1], (DEC_BATCH, DEC_SEQ, D_MODEL), 1.0),
        "state_conv": nrm(ks[2], (DEPTH, DEC_BATCH, CONV_W - 1, D_RNN), 1.0),
        "state_h": nrm(ks[3], (DEPTH, DEC_BATCH, D_RNN), 0.5),
        "state_pool": nrm(ks[4], (DEPTH, DEC_BATCH, POOL_MAX - 1, D_POOL), 1.0),
        "norm_gains": 1.0 + nrm(ks[5], (DEPTH, 6, D_MODEL), 0.05),
        "w_ffn_in": nrm(ks[6], (DEPTH, 2, D_MODEL, 2 * D_FF), D_MODEL ** -0.5),
        "w_ffn_out": nrm(ks[7], (DEPTH, 2, D_FF, D_MODEL), D_FF ** -0.5),
        "w_in": nrm(ks[8], (DEPTH, D_MODEL, IN_COLS), D_MODEL ** -0.5),
        "conv_w": nrm(ks[9], (DEPTH, CONV_W, D_RNN), CONV_W ** -0.5),
        "conv_b": nrm(ks[10], (DEPTH, D_RNN), 0.01),
        "w_rg_a": nrm(ks[11], (DEPTH, N_RG_HEADS, RG_BLOCK, RG_BLOCK), RG_BLOCK ** -0.5),
        "b_rg_a": nrm(ks[13], (DEPTH, D_RNN), 0.01),
        "w_rg_x": nrm(ks[14], (DEPTH, N_RG_HEADS, RG_BLOCK, RG_BLOCK), RG_BLOCK ** -0.5),
        "b_rg_x": nrm(ks[15], (DEPTH, D_RNN), 0.01),
        "lru_param": jnp.log(u) - jnp.log1p(-u),
        "w_pool_mix": nrm(ks[16], (DEPTH, N_POOL_GROUPS, POOL_GROUP, POOL_GROUP), POOL_GROUP ** -0.5),
        "pool_scale": 1.0 + nrm(ks[17], (DEPTH, D_POOL), 0.05),
        "w_br_rg": nrm(ks[18], (DEPTH, D_RNN, D_MODEL), D_RNN ** -0.5),
        "w_br_pool": nrm(ks[19], (DEPTH, D_POOL, D_MODEL), D_POOL ** -0.5),
        "w_out": nrm(ks[20], (DEPTH, D_MODEL, D_MODEL), D_MODEL ** -0.5),
    }


def reference(x_prompt, x_sample, state_conv, state_h, state_pool, norm_gains, w_ffn_in, w_ffn_out, w_in,
              conv_w, conv_b, w_rg_a, b_rg_a, w_rg_x, b_rg_x, lru_param, w_pool_mix, pool_scale,
              w_br_rg, w_br_pool, w_out):
    Bp = x_prompt.shape[0]
    conv_p, h_p, pool_p = [], [], []
    conv_s, h_s, pool_s = [], [], []
    yp = x_prompt
    ys = x_sample
    for l in range(DEPTH):
        params = (norm_gains[l], w_ffn_in[l], w_ffn_out[l], w_in[l], conv_w[l], conv_b[l],
                  w_rg_a[l], b_rg_a[l], w_rg_x[l], b_rg_x[l], lru_param[l], w_pool_mix[l],
                  pool_scale[l], w_br_rg[l], w_br_pool[l], w_out[l])
        yp, c_new, h_new, p_new = layer(
            yp,
            jnp.zeros((Bp, CONV_W - 1, D_RNN), yp.dtype),
            jnp.zeros((Bp, D_RNN), jnp.float32),
            jnp.zeros((Bp, POOL_MAX - 1, D_POOL), yp.dtype),
            0, *params)
        conv_p.append(c_new)
        h_p.append(h_new)
        pool_p.append(p_new)
        ys, c_new, h_new, p_new = layer(ys, state_conv[l], state_h[l], state_pool[l], POOL_MAX - 1, *params)
        conv_s.append(c_new)
        h_s.append(h_new)
        pool_s.append(p_new)
    new_conv_prompt = jnp.stack(conv_p)
    new_h_prompt = jnp.stack(h_p)
    new_pool_prompt = jnp.stack(pool_p)
    new_conv_sample = jnp.stack(conv_s)
    new_h_sample = jnp.stack(h_s)
    new_pool_sample = jnp.stack(pool_s)
    return (yp, ys, new_conv_prompt, new_h_prompt, new_pool_prompt, new_conv_sample, new_h_sample, new_pool_sample)
```

```python
import numpy as np
from contextlib import ExitStack
import concourse.bass as bass
import concourse.mybir as mybir
from concourse.bass_utils import run_bass_kernel_spmd

F32 = mybir.dt.float32
BF16 = mybir.dt.bfloat16
AF = mybir.ActivationFunctionType
ALU = mybir.AluOpType

D = 2048
NCH = 16
DFF = 6144
NHC = 48
DP = 1024
NPC = 8
EPS = 1e-6
TMAX = 576
TW = 608
NSLOT = 4
SLOTE = 4096
WINS = (2, 4, 8, 16)
NPV = 232
PV_G, PV_CW, PV_CB, PV_BA, PV_BX, PV_LAM, PV_PS = 0, 96, 160, 176, 192, 208, 224
RSTD_MODE = "lnexp"
NST = 48 + 16 + 120


class Op:
    __slots__ = ("eng", "fn", "deps", "sig", "idx", "cnt", "dsem", "dval", "tag")


class Prog:
    ENG = ("pe", "act", "dve", "pool", "sp")

    def __init__(self):
        self.ops = {e: [] for e in self.ENG}
        self.lastw = {}
        self.readers = {}
        self.tag = ""
        self.annotate = False

    def add(self, eng, fn, reads=(), writes=(), dsem=None, dval=0, extra_deps=()):
        op = Op()
        op.eng, op.fn, op.sig, op.idx, op.cnt = eng, fn, False, len(self.ops[eng]), 0
        op.dsem, op.dval = dsem, dval
        op.tag = self.tag
        deps = set(extra_deps)
        for r in reads:
            w = self.lastw.get(r)
            if w is not None:
                deps.add(w)
        for w_ in writes:
            lw = self.lastw.get(w_)
            if lw is not None:
                deps.add(lw)
            for rd in self.readers.get(w_, ()):
                deps.add(rd)
        for r in reads:
            self.readers.setdefault(r, []).append(op)
        for w_ in writes:
            self.lastw[w_] = op
            self.readers[w_] = []
        deps.discard(op)
        keep = {}
        for d in deps:
            if d.dsem is not None:
                keep[("dma", id(d))] = d
                continue
            if d.eng == eng:
                if eng == "pe":
                    continue
                if op.idx - d.idx > 2:
                    continue
            k = d.eng
            if k not in keep or keep[k].idx < d.idx:
                keep[k] = d
        op.deps = list(keep.values())
        for d in op.deps:
            if d.dsem is None:
                d.sig = True
        self.ops[eng].append(op)
        return op

    def emit(self, eng, handle, sems):
        waited = {}
        for op in self.ops[eng]:
            for d in op.deps:
                if d.dsem is not None:
                    key, s, v = d.dsem[0], d.dsem[1], d.dval
                else:
                    key, s, v = d.eng, sems[d.eng], d.cnt
                if waited.get(key, -1) < v:
                    handle.wait_ge(s, v)
                    waited[key] = v
            ins = op.fn(handle)
            if self.annotate and op.tag:
                ins.annotate(op.tag)
            if op.sig:
                ins.then_inc(sems[eng], 1)

    def finalize(self):
        for e in self.ENG:
            c = 0
            for op in self.ops[e]:
                if op.sig:
                    c += 1
                    op.cnt = c


def make_tiles(p_len, s_len, tile=512):
    tiles = []
    nt = p_len // tile
    for i in range(nt):
        blks = [(0, i * tile, tile, 0)]
        if i == nt - 1 and s_len > 0:
            blks.append((1, 0, s_len, tile))
        tiles.append(blks)
    return tiles


def build_nc(depth, p_len, s_len, annotate=False):
    nc = bass.Bass("TRN2", target_bir_lowering=False)
    ttot = p_len + s_len
    dt = lambda name, shape, kind: nc.dram_tensor(name, shape, F32, kind=kind).ap()
    xT = dt("xT", [D, ttot], "ExternalInput")
    pvec = dt("pvec", [128, depth * NPV], "ExternalInput")
    sin = dt("sin", [128, depth * NST], "ExternalInput")
    w_ffn_in = dt("w_ffn_in", [depth, 2, D, 2 * DFF], "ExternalInput")
    w_ffn_out = dt("w_ffn_out", [depth, 2, DFF, D], "ExternalInput")
    w_in = dt("w_in", [depth, D, 9216], "ExternalInput")
    w_rg_a = dt("w_rg_a", [depth, 16, 128, 128], "ExternalInput")
    w_rg_x = dt("w_rg_x", [depth, 16, 128, 128], "ExternalInput")
    w_pool_mix = dt("w_pool_mix", [depth, 4, 256, 256], "ExternalInput")
    w_br_rg = dt("w_br_rg", [depth, D, D], "ExternalInput")
    w_br_pool = dt("w_br_pool", [depth, DP, D], "ExternalInput")
    w_out = dt("w_out", [depth, D, D], "ExternalInput")
    yT = dt("yT", [D, ttot], "ExternalOutput")
    sout = dt("sout", [128, depth * 2 * NST], "ExternalOutput")

    tiles = make_tiles(p_len, s_len)
    P = Prog()
    P.annotate = annotate

    with ExitStack() as es:
        sb = lambda name, shape, dtype: es.enter_context(nc.sbuf_tensor(name, shape, dtype))
        xs = sb("xs", [128, NCH, TMAX], F32)
        outb = sb("outb", [128, NCH, TMAX], F32)
        xnv = outb.bitcast(BF16)
        hb = sb("hb", [128, NHC, TMAX], BF16)
        wring = sb("wring", [128, NSLOT, SLOTE], BF16)
        tmp = sb("tmp", [128, 11, TW], F32)
        sqb = sb("sqb", [128, 2, TMAX], BF16)
        xcb = sb("xcb", [128, 2, TMAX], BF16)
        rs = sb("rs", [128, TMAX], F32)
        rstd = sb("rstd", [128, TMAX], F32)
        pv = sb("pv", [128, depth * NPV], F32)
        hgc = sb("hgc", [128, depth, 3, 16], F32)
        dpar = sb("dpar", [128, depth, 3, 16], F32)
        state = sb("state", [128, depth * 2 * NST], F32)
        ones = sb("ones", [128, 128], BF16)
        icnt = sb("icnt", [128, 4, 16], F32)
        zs = sb("zs", [128, 8], F32)
        zsp = sb("zsp", [128, 3, 32], F32)
        xcs = sb("xcs", [128, 4], F32)
        psum = es.enter_context(nc.psum_tensor("psum", [128, 4, 1024], F32))
        sem = lambda name: es.enter_context(nc.semaphore(name))
        sems = {e: sem("s_" + e) for e in ("pe", "act", "dve", "pool")}
        wsem = [("w%d" % i, sem("s_w%d" % i)) for i in range(NSLOT)]
        msem = [("m%d" % i, sem("s_m%d" % i)) for i in range(8)]
        wfill = [0] * NSLOT
        mcount = [0] * 8
        mnext = [0]
        wnext = [0]

        def xn(k, lo=0, hi=TMAX):
            return xnv[:, k // 2, (k % 2) * TMAX + lo:(k % 2) * TMAX + hi]

        def R_out(c):
            return [("xn", 2 * c), ("xn", 2 * c + 1)] if c < 8 else [("out", c)]

        def misc_dma(out, in_, reads=(), writes=()):
            i = mnext[0] % 8
            mnext[0] += 1
            key, s = msem[i]
            prev = mcount[i]
            mcount[i] += 16
            val = mcount[i]

            def fn(e, out=out, in_=in_, s=s, prev=prev):
                if prev:
                    e.wait_ge(s, prev)
                return e.dma_start(out=out, in_=in_).then_inc(s, 16)
            return P.add("sp", fn, reads, writes, dsem=(key, s), dval=val)

        def wfill_op(pairs):
            slot = wnext[0] % NSLOT
            wnext[0] += 1
            key, s = wsem[slot]
            wfill[slot] += 16 * len(pairs)
            val = wfill[slot]
            prs = [(mk(slot), src) for mk, src in pairs]

            def fn(e, prs=prs, s=s):
                ins = None
                for dst, src in prs:
                    ins = e.dma_start(out=dst, in_=src).then_inc(s, 16)
                return ins
            op = P.add("pool", fn, (), [("w", slot)], dsem=(key, s), dval=val)
            return slot

        def wview(slot, off, nk, ncol):
            return wring[:, slot, off:off + nk * ncol].rearrange("p (k n) -> p k n", n=ncol)

        def ACT(out, in_, func, reads, writes, scale=None, bias=None):
            kw = {}
            if scale is not None:
                kw["scale"] = scale
            if bias is not None:
                kw["bias"] = bias

            def fn(e):
                return e.activation(out=out, in_=in_, func=func, **kw)
            return P.add("act", fn, reads, writes)

        def TS1(out, in0, s1, op, reads, writes):
            return P.add("dve", lambda e: e.tensor_single_scalar(out=out, in_=in0, scalar=s1, op=op), reads, writes)

        def TT(out, in0, in1, op, reads, writes):
            return P.add("dve", lambda e: e.tensor_tensor(out=out, in0=in0, in1=in1, op=op), reads, writes)

        def STT(out, in0, scalar, in1, op0, op1, reads, writes):
            return P.add("dve", lambda e: e.scalar_tensor_tensor(out=out, in0=in0, scalar=scalar, in1=in1, op0=op0, op1=op1), reads, writes)

        def TS(out, in0, s1, s2, op0, op1, reads, writes):
            return P.add("dve", lambda e: e.tensor_scalar(out=out, in0=in0, scalar1=s1, scalar2=s2, op0=op0, op1=op1), reads, writes)

        def CP(out, in_, reads, writes):
            return P.add("dve", lambda e: e.tensor_copy(out, in_), reads, writes)

        def MM(slot_ps, mms, reads, writes, first=True, last=True):
            blks = cur["blks"]
            n = len(mms)
            if cur.get("fine") and n == NCH and ("xn", 0) in reads:
                cur["fine"] = False
                op = None
                for i, (lhsT, rhsf) in enumerate(mms):
                    def fn1(e, i=i, lhsT=lhsT, rhsf=rhsf):
                        ins = None
                        for (_, _, nn, c0) in blks:
                            ins = e.matmul(psum[:, slot_ps, c0:c0 + nn], lhsT=lhsT, rhs=rhsf(c0, c0 + nn),
                                           start=(first and i == 0), stop=(last and i == n - 1))
                        return ins
                    rd = [r for r in reads if not (isinstance(r, tuple) and r[0] == "xn")] + [("xn", i)]
                    op = P.add("pe", fn1, rd, list(writes) + [("ps", slot_ps)])
                return op

            def fn(e):
                ins = None
                for i, (lhsT, rhsf) in enumerate(mms):
                    for (_, _, nn, c0) in blks:
                        ins = e.matmul(psum[:, slot_ps, c0:c0 + nn], lhsT=lhsT, rhs=rhsf(c0, c0 + nn),
                                       start=(first and i == 0), stop=(last and i == n - 1))
                return ins
            return P.add("pe", fn, reads, list(writes) + [("ps", slot_ps)])

        cur = {}

        def ps_blocks(slot):
            return [(psum[:, slot, c0:c0 + nn], c0, c0 + nn) for (_, _, nn, c0) in cur["blks"]]

        misc_dma(pv[:, :], pvec[:, :], (), ["pv"])
        P.add("dve", lambda e: e.memset(state[:, :], 0.0), (), ["state"])
        P.add("dve", lambda e: e.memset(ones[:, :], 1.0), (), ["ones"])
        P.add("dve", lambda e: e.memset(xcs[:, :], 0.0), (), ["xcs", "dummy_tanh"])
        for wi, w in enumerate(WINS):
            for t in range(16):
                P.add("dve", lambda e, wi=wi, t=t, w=w: e.memset(icnt[:, wi, t:t + 1], 1.0 / min(t + 1, w)), (), ["icnt"])
        if s_len > 0:
            for l in range(depth):
                so = (l * 2 + 1) * NST
                misc_dma(state[:, so:so + NST], sin[:, l * NST:(l + 1) * NST], (), ["state"])
        for l in range(depth):
            b = l * NPV
            TS1(dpar[:, l, 0, :], pv[:, b + PV_BA:b + PV_BA + 16], 0.5, ALU.mult, ["pv"], ["dpar"])
            for hi_, (gi_, sc_) in enumerate(((1, 0.5), (3, 1.0), (5, 0.5))):
                TS1(hgc[:, l, hi_, :], pv[:, b + PV_G + gi_ * 16:b + PV_G + gi_ * 16 + 16], sc_, ALU.mult, ["pv"], ["hgc"])
            TS1(dpar[:, l, 1, :], pv[:, b + PV_BX:b + PV_BX + 16], 0.5, ALU.mult, ["pv"], ["dpar"])
            ACT(dpar[:, l, 2, :], pv[:, b + PV_LAM:b + PV_LAM + 16], AF.Exp, ["pv"], ["dpar2"], scale=-1.0)
            ACT(dpar[:, l, 2, :], dpar[:, l, 2, :], AF.Ln, ["dpar2"], ["dpar2"], scale=1.0, bias=1.0)
            TS1(dpar[:, l, 2, :], dpar[:, l, 2, :], -4.0, ALU.mult, ["dpar2"], ["dpar", "dpar2"])

        def st_off(l, seq):
            return (l * 2 + seq) * NST

        def pcol(l, off, c):
            return pv[:, l * NPV + off + c:l * NPV + off + c + 1]

        pending_stats = []

        def flush_stats():
            while pending_stats:
                pending_stats.pop(0)()

        def stats_add(src_ap, src_keys, k, n, T, on_act=False, delay=False):
            b = k % 2
            if on_act:
                ACT(sqb[:, b, 0:T], src_ap, AF.Square, src_keys, [("sqb", b)])
            else:
                TT(sqb[:, b, 0:T], src_ap, src_ap, ALU.mult, src_keys, [("sqb", b)])
            mm = lambda: MM(3, [(ones[:, :], lambda lo, hi, b=b: sqb[:, b, lo:hi])], [("sqb", b), "ones"], [], first=(k == 0), last=(k == n - 1))
            if delay:
                pending_stats.append(mm)
            else:
                mm()

        def rstd_from_stats(T, eps):
            if RSTD_MODE == "arsqrt":
                for (pa, lo, hi) in ps_blocks(3):
                    ACT(rstd[:, lo:hi], pa, AF.Abs_reciprocal_sqrt, [], ["rstd", ("ps", 3)], scale=1.0 / D, bias=eps)
            elif RSTD_MODE == "lnexp":
                for (pa, lo, hi) in ps_blocks(3):
                    ACT(rs[:, lo:hi], pa, AF.Ln, [], ["rs", ("ps", 3)], scale=1.0 / D, bias=eps)
                ACT(rstd[:, 0:T], rs[:, 0:T], AF.Exp, ["rs"], ["rstd"], scale=-0.5)
            else:
                for (pa, lo, hi) in ps_blocks(3):
                    ACT(rs[:, lo:hi], pa, AF.Sqrt, [], ["rs", ("ps", 3)], scale=1.0 / D, bias=eps)
                P.add("dve", lambda e: e.reciprocal(out=rstd[:, 0:T], in_=rs[:, 0:T]), ["rs"], ["rstd"])

        def prenorm(l, gi, T):
            P.tag = 'prenorm'
            for c in range(NCH):
                stats_add(xs[:, c, 0:T], [("x", c)], c, NCH, T, on_act=True)
            rstd_from_stats(T, EPS)
            for c in range(NCH):
                STT(xn(c, 0, T), xs[:, c, 0:T], pcol(l, PV_G + gi * 16, c), rstd[:, 0:T], ALU.mult, ALU.mult,
                    [("x", c), "rstd", "pv"], [("xn", c)])
            cur["fine"] = True

        POOL_CHUNKS = (12, 13, 14, 15)

        def postnorm(l, gi, T, eps, half):
            P.tag = 'postnorm'
            rstd_from_stats(T, eps)
            for c in range(NCH):
                if c in POOL_CHUNKS:
                    tb = 7 + (c % 2)
                    P.add("pool", lambda e, c=c, tb=tb: e.tensor_tensor(out=tmp[:, tb, 0:T], in0=outb[:, c, 0:T], in1=rstd[:, 0:T], op=ALU.mult),
                          R_out(c) + ["rstd"], [("tmp", tb)])
                    P.add("pool", lambda e, c=c, tb=tb: e.tensor_tensor(out=xs[:, c, 0:T], in0=tmp[:, tb, 0:T], in1=xs[:, c, 0:T], op=ALU.add),
                          [("tmp", tb), ("x", c)], [("x", c)])
                else:
                    tb = 4 + (c % 2)
                    TT(tmp[:, tb, 0:T], outb[:, c, 0:T], rstd[:, 0:T], ALU.mult, R_out(c) + ["rstd"], [("tmp", tb)])
                    TT(xs[:, c, 0:T], tmp[:, tb, 0:T], xs[:, c, 0:T], ALU.add, [("tmp", tb), ("x", c)], [("x", c)])

        def out_evac(slot, f, T, l, hidx):
            b = f % 2
            for (pa, lo, hi) in ps_blocks(slot):
                ACT(sqb[:, b, lo:hi], pa, AF.Square, [], [("sqb", b), ("ps", slot)])
            for (pa, lo, hi) in ps_blocks(slot):
                ACT(outb[:, f, lo:hi], pa, AF.Copy, ["hgc"], R_out(f) + [("ps", slot)], scale=hgc[:, l, hidx, f:f + 1])
            pending_stats.append(lambda: MM(3, [(ones[:, :], lambda lo, hi, b=b: sqb[:, b, lo:hi])], [("sqb", b), "ones"], [],
                                            first=(f == 0), last=(f == NCH - 1)))

        def ffn(l, i, T):
            Win = w_ffn_in[l, i].rearrange("(kc p) n -> p kc n", p=128)
            Wout = w_ffn_out[l, i].rearrange("(kc p) n -> p kc n", p=128)
            prenorm(l, 0 if i == 0 else 4, T)
            P.tag = 'ffn1'
            for j in range(NHC):
                slot = wfill_op([
                    (lambda s: wview(s, 0, 16, 128), Win[:, :, j * 128:(j + 1) * 128]),
                    (lambda s: wview(s, 2048, 16, 128), Win[:, :, DFF + j * 128:DFF + (j + 1) * 128]),
                ])
                pa, pb = (0, 1) if j % 2 == 0 else (2, 3)
                wg = wview(slot, 0, 16, 128)
                wu = wview(slot, 2048, 16, 128)
                xr = [("xn", k) for k in range(NCH)]
                MM(pa, [(wg[:, k, :], lambda lo, hi, k=k: xn(k, lo, hi)) for k in range(NCH)], xr + [("w", slot)], [])
                MM(pb, [(wu[:, k, :], lambda lo, hi, k=k: xn(k, lo, hi)) for k in range(NCH)], xr + [("w", slot)], [])
                tb = 2 + (j % 2)
                for (pap, lo, hi) in ps_blocks(pa):
                    ACT(tmp[:, tb, lo:hi], pap, AF.Silu, [], [("tmp", tb), ("ps", pa)])
                for (pbp, lo, hi) in ps_blocks(pb):
                    TT(hb[:, j, lo:hi], tmp[:, tb, lo:hi], pbp, ALU.mult, [("tmp", tb)], [("h", j), ("ps", pb)])
            P.tag = 'ffn2'
            for f in range(NCH):
                po = f % 3
                for half in range(2):
                    slot = wfill_op([(lambda s: wview(s, 0, 24, 128), Wout[:, half * 24:(half + 1) * 24, f * 128:(f + 1) * 128])])
                    wv = wview(slot, 0, 24, 128)
                    MM(po, [(wv[:, k, :], lambda lo, hi, kk=half * 24 + k: hb[:, kk, lo:hi]) for k in range(24)],
                       [("h", half * 24 + k) for k in range(24)] + [("w", slot)], [], first=(half == 0), last=(half == 1))
                flush_stats()
                out_evac(po, f, T, l, 0 if i == 0 else 2)
            flush_stats()
            postnorm(l, 1 if i == 0 else 5, T, EPS, True)

        def mixer(l, T, first_tile):
            blks = cur["blks"]
            Win = w_in[l].rearrange("(kc p) n -> p kc n", p=128)
            prenorm(l, 2, T)
            xr = [("xn", k) for k in range(NCH)]
            ZD, XC, A1, A2, A3, A4, A5 = 0, 1, 2, 3, 4, 5, 6
            XCs, A5s = (1, 9), (6, 10)
            hbf = hb.bitcast(F32)
            ysbs = [hbf[:, 40 + 2 * i:42 + 2 * i, :].rearrange("p a b -> p (a b)") for i in range(2)]
            ginfo = {}

            def stage_pe_xy(c):
                P.tag = 'M1xy%d' % c
                slot = wfill_op([
                    (lambda s: wview(s, 0, 16, 128), Win[:, :, c * 128:(c + 1) * 128]),
                    (lambda s: wview(s, 2048, 16, 128), Win[:, :, D + c * 128:D + (c + 1) * 128]),
                ])
                w1 = wview(slot, 0, 16, 128)
                w2 = wview(slot, 2048, 16, 128)
                MM(0, [(w1[:, k, :], lambda lo, hi, k=k: xn(k, lo, hi)) for k in range(NCH)], xr + [("w", slot)], [])
                MM(1, [(w2[:, k, :], lambda lo, hi, k=k: xn(k, lo, hi)) for k in range(NCH)], xr + [("w", slot)], [])

            def stage_conv(c):
                P.tag = 'M1conv%d' % c
                XC = XCs[c % 2]
                ct0 = st_off(l, 0) + c * 3
                CP(tmp[:, ZD, 0:3], state[:, ct0:ct0 + 3], ["state"], ["zpre"])
                for (pap, lo, hi) in ps_blocks(0):
                    ACT(tmp[:, ZD, 3 + lo:3 + hi], pap, AF.Copy, [], ["zdat", ("ps", 0)])
                cw = lambda k: pcol(l, PV_CW + k * 16, c)
                TS(tmp[:, XC, 0:T], tmp[:, ZD, 0:T], cw(0), pcol(l, PV_CB, c), ALU.mult, ALU.add, ["zpre", "zdat", "pv"], [("tmp", XC)])
                for k in range(1, 4):
                    STT(tmp[:, XC, 0:T], tmp[:, ZD, k:k + T], cw(k), tmp[:, XC, 0:T], ALU.mult, ALU.add, ["zpre", "zdat", "pv", ("tmp", XC)], [("tmp", XC)])
                n0 = blks[0][2]
                if len(blks) > 1:
                    ct1 = st_off(l, 1) + c * 3
                    CP(zs[:, 0:3], state[:, ct1:ct1 + 3], ["state"], ["zs"])
                    CP(zs[:, 3:6], tmp[:, ZD, 3 + n0:3 + n0 + 3], ["zdat"], ["zs"])
                    TS(xcs[:, 0:3], zs[:, 0:3], cw(0), pcol(l, PV_CB, c), ALU.mult, ALU.add, ["zs", "pv"], ["xcs"])
                    for k in range(1, 4):
                        STT(xcs[:, 0:3], zs[:, k:k + 3], cw(k), xcs[:, 0:3], ALU.mult, ALU.add, ["zs", "pv", "xcs"], ["xcs"])
                    CP(tmp[:, XC, n0:n0 + 3], xcs[:, 0:3], ["xcs", ("tmp", XC)], [("tmp", XC)])
                    CP(state[:, ct1:ct1 + 3], tmp[:, ZD, 3 + T - 3:3 + T], ["zdat", "zs"], ["state"])
                CP(state[:, ct0:ct0 + 3], tmp[:, ZD, 3 + n0 - 3:3 + n0], ["zdat", "zpre"], ["state"])

            def stage_gelu(c):
                P.tag = 'M1gelu%d' % c
                A5 = A5s[c % 2]
                ysb = ysbs[c % 2]
                for (pbp, lo, hi) in ps_blocks(1):
                    ACT(ysb[:, lo:hi], pbp, AF.Copy, [], [("ysb", c % 2), ("ps", 1)])
                ACT(tmp[:, A5, 0:T], ysb[:, 0:T], AF.Square, [("ysb", c % 2)], [("tmp", A5)], scale=0.044715 ** 0.5)
                STT(tmp[:, A5, 0:T], tmp[:, A5, 0:T], 1.0, ysb[:, 0:T], ALU.add, ALU.mult, [("tmp", A5), ("ysb", c % 2)], [("tmp", A5)])
                ACT(tmp[:, A5, 0:T], tmp[:, A5, 0:T], AF.Tanh, [("tmp", A5)], [("tmp", A5)], scale=0.7978845608028654)
                STT(tmp[:, A5, 0:T], tmp[:, A5, 0:T], 1.0, ysb[:, 0:T], ALU.add, ALU.mult, [("tmp", A5), ("ysb", c % 2)], [("tmp", A5)])

            def stage_xcb(c):
                P.tag = 'M1xcb%d' % c
                XC, xb = XCs[c % 2], c % 2
                CP(xcb[:, xb, 0:T], tmp[:, XC, 0:T], [("tmp", XC)], [("xcb", xb)])

            def stageB(c):
                P.tag = 'M1B%d' % c
                XC, A5, xb = XCs[c % 2], A5s[c % 2], c % 2
                gslot = wfill_op([
                    (lambda s: wring[:, s, 0:128], w_rg_a[l, c]),
                    (lambda s: wring[:, s, 128:256], w_rg_x[l, c]),
                ])
                wa_c = wring[:, gslot, 0:128]
                wx_c = wring[:, gslot, 128:256]
                MM(2, [(wa_c, lambda lo, hi: xcb[:, xb, lo:hi])], [("xcb", xb), ("w", gslot)], [])
                MM(3, [(wx_c, lambda lo, hi: xcb[:, xb, lo:hi])], [("xcb", xb), ("w", gslot)], [])
                for (pap, lo, hi) in ps_blocks(2):
                    ACT(tmp[:, A1, lo:hi], pap, AF.Tanh, ["dpar"], [("tmp", A1), ("ps", 2)], scale=0.5, bias=dpar[:, l, 0, c:c + 1])
                ACT(tmp[:, A1, 0:T], tmp[:, A1, 0:T], AF.Exp, [("tmp", A1), "dpar"], [("tmp", A1)], scale=dpar[:, l, 2, c:c + 1], bias=dpar[:, l, 2, c:c + 1])
                ACT(tmp[:, A2, 0:T], tmp[:, A1, 0:T], AF.Square, [("tmp", A1)], [("tmp", A2)])
                for (pap, lo, hi) in ps_blocks(3):
                    ACT(tmp[:, A3, lo:hi], pap, AF.Tanh, ["dpar"], [("tmp", A3), ("ps", 3)], scale=0.5, bias=dpar[:, l, 1, c:c + 1])
                ACT(tmp[:, A2, 0:T], tmp[:, A2, 0:T], AF.Sqrt, [("tmp", A2)], [("tmp", A2)], scale=-1.0, bias=1.0)
                ACT(xcs[:, 3:4], xcs[:, 3:4], AF.Tanh, [], ["dummy_tanh"])
                STT(tmp[:, A3, 0:T], tmp[:, A3, 0:T], 1.0, tmp[:, XC, 0:T], ALU.add, ALU.mult, [("tmp", A3), ("tmp", XC)], [("tmp", A3)])
                STT(tmp[:, A3, 0:T], tmp[:, A3, 0:T], 0.5, tmp[:, A2, 0:T], ALU.mult, ALU.mult, [("tmp", A3), ("tmp", A2)], [("tmp", A3)])
                for (seq, _, nn, c0) in blks:
                    hcol = st_off(l, seq) + 48 + c
                    P.add("dve", lambda e, c0=c0, nn=nn, hcol=hcol: e.tensor_tensor_scan(
                        out=tmp[:, A4, c0:c0 + nn], data0=tmp[:, A1, c0:c0 + nn], data1=tmp[:, A3, c0:c0 + nn],
                        initial=state[:, hcol:hcol + 1], op0=ALU.mult, op1=ALU.add),
                        [("tmp", A1), ("tmp", A3), "state"], [("tmp", A4)])
                    CP(state[:, hcol:hcol + 1], tmp[:, A4, c0 + nn - 1:c0 + nn], [("tmp", A4)], ["state"])
                STT(hb[:, c, 0:T], tmp[:, A4, 0:T], 0.5, tmp[:, A5, 0:T], ALU.mult, ALU.mult, [("tmp", A4), ("tmp", A5)], [("h", c)])

            for c in range(NCH + 1):
                if c < NCH:
                    stage_pe_xy(c)
                    stage_conv(c)
                    stage_gelu(c)
                if c >= 1:
                    stageB(c - 1)
                if c < NCH:
                    stage_xcb(c)
            P1, P2 = 7, 8
            P.tag = 'M2'
            for pc in range(NPC):
                g = pc // 2
                w = WINS[g]
                m = g + 1
                if pc % 2 == 0:
                    pslot = wfill_op([(lambda s: wview(s, 0, 16, 256), Win[:, :, 2 * D + pc * 128:2 * D + (pc + 2) * 128])])
                wp = wview(pslot, 0, 16, 256)
                o = (pc % 2) * 128
                MM(0, [(wp[:, k, o:o + 128], lambda lo, hi, k=k: xn(k, lo, hi)) for k in range(NCH)], xr + [("w", pslot)], [])
                pt0 = st_off(l, 0) + 64 + pc * 15
                n0 = blks[0][2]
                CP(tmp[:, ZD, 0:15], state[:, pt0:pt0 + 15], ["state"], ["zpre"])
                for (pap, lo, hi) in ps_blocks(0):
                    ACT(tmp[:, ZD, 15 + lo:15 + hi], pap, AF.Copy, [], ["zdat", ("ps", 0)])
                L = 15 + T
                src, sk = tmp[:, ZD], ["zpre", "zdat"]
                k = 1
                bufs = [P1, P2]
                for step in range(m):
                    dst = bufs[step % 2]
                    TT(tmp[:, dst, k:L], src[:, k:L], src[:, 0:L - k], ALU.add, sk, [("tmp", dst)])
                    src, sk = tmp[:, dst], [("tmp", dst)]
                    k *= 2
                STT(hb[:, 16 + pc, 0:T], src[:, 15:15 + T], 1.0 / w, tmp[:, ZD, 15:15 + T], ALU.mult, ALU.subtract,
                    sk + ["zdat"], [("h", 16 + pc)])
                if first_tile:
                    wi = g
                    TT(zsp[:, 2, 0:15], src[:, 15:30], icnt[:, wi, 0:15], ALU.mult, sk + ["icnt"], ["zsp2"])
                    TT(hb[:, 16 + pc, 0:15], zsp[:, 2, 0:15], tmp[:, ZD, 15:30], ALU.subtract, ["zsp2", "zdat", ("h", 16 + pc)], [("h", 16 + pc)])
                if len(blks) > 1:
                    pt1 = st_off(l, 1) + 64 + pc * 15
                    CP(zsp[:, 0, 0:15], state[:, pt1:pt1 + 15], ["state"], ["zsp0"])
                    CP(zsp[:, 0, 15:30], tmp[:, ZD, 15 + n0:15 + n0 + 15], ["zdat"], ["zsp0"])
                    s2, k2 = 0, 1
                    for step in range(m):
                        d2 = 1 + (step % 2)
                        TT(zsp[:, d2, k2:30], zsp[:, s2, k2:30], zsp[:, s2, 0:30 - k2], ALU.add, ["zsp%d" % s2], ["zsp%d" % d2])
                        s2 = d2
                        k2 *= 2
                    STT(hb[:, 16 + pc, n0:n0 + 15], zsp[:, s2, 15:30], 1.0 / w, zsp[:, 0, 15:30], ALU.mult, ALU.subtract,
                        ["zsp%d" % s2, "zsp0", ("h", 16 + pc)], [("h", 16 + pc)])
                    CP(state[:, pt1:pt1 + 15], tmp[:, ZD, 15 + T - 15:15 + T], ["zdat", "zsp0"], ["state"])
                CP(state[:, pt0:pt0 + 15], tmp[:, ZD, 15 + n0 - 15:15 + n0], ["zdat", "zpre"], ["state"])
            mslot = wfill_op([(lambda s: wring[:, s, 0:2048].rearrange("p (g k d) -> p g k d", g=4, k=2),
                               w_pool_mix[l].rearrange("g (k p) d -> p g k d", p=128))])
            wm = wring[:, mslot, 0:2048].rearrange("p (g k d) -> p g k d", g=4, k=2)
            for g in range(4):
                for oc in range(2):
                    sl = 1 + ((g * 2 + oc) % 2)
                    MM(sl, [(wm[:, g, k, oc * 128:(oc + 1) * 128], lambda lo, hi, kk=16 + 2 * g + k: hb[:, kk, lo:hi]) for k in range(2)],
                       [("h", 16 + 2 * g), ("h", 17 + 2 * g), ("w", mslot)], [])
                    for (pap, lo, hi) in ps_blocks(sl):
                        ACT(hb[:, 24 + 2 * g + oc, lo:hi], pap, AF.Identity, ["pv"], [("h", 24 + 2 * g + oc), ("ps", sl)],
                            scale=pcol(l, PV_PS, 2 * g + oc))
            P.tag = 'M3'
            Wbr = w_br_rg[l].rearrange("(kc p) n -> p kc n", p=128)
            Wbp = w_br_pool[l].rearrange("(kc p) n -> p kc n", p=128)
            for f in range(NCH):
                s1 = wfill_op([
                    (lambda s: wview(s, 0, 16, 128), Win[:, :, 5120 + f * 128:5120 + (f + 1) * 128]),
                    (lambda s: wview(s, 2048, 16, 128), Win[:, :, 7168 + f * 128:7168 + (f + 1) * 128]),
                ])
                s2_ = wfill_op([
                    (lambda s: wview(s, 0, 16, 128), Wbr[:, :, f * 128:(f + 1) * 128]),
                    (lambda s: wview(s, 2048, 8, 128), Wbp[:, :, f * 128:(f + 1) * 128]),
                ])
                wgr, wgp = wview(s1, 0, 16, 128), wview(s1, 2048, 16, 128)
                wbr, wbp = wview(s2_, 0, 16, 128), wview(s2_, 2048, 8, 128)
                MM(0, [(wgr[:, k, :], lambda lo, hi, k=k: xn(k, lo, hi)) for k in range(NCH)], xr + [("w", s1)], [])
                MM(1, [(wgp[:, k, :], lambda lo, hi, k=k: xn(k, lo, hi)) for k in range(NCH)], xr + [("w", s1)], [])
                MM(2, [(wbr[:, k, :], lambda lo, hi, k=k: hb[:, k, lo:hi]) for k in range(NCH)], [("h", k) for k in range(NCH)] + [("w", s2_)], [])
                MM(3, [(wbp[:, k, :], lambda lo, hi, k=k: hb[:, 24 + k, lo:hi]) for k in range(NPC)], [("h", 24 + k) for k in range(NPC)] + [("w", s2_)], [])
                for (pap, lo, hi) in ps_blocks(0):
                    ACT(tmp[:, 2, lo:hi], pap, AF.Tanh, [], [("tmp", 2), ("ps", 0)], scale=0.5)
                for (pap, lo, hi) in ps_blocks(1):
                    ACT(tmp[:, 3, lo:hi], pap, AF.Tanh, [], [("tmp", 3), ("ps", 1)], scale=0.5)
                for (pap, lo, hi) in ps_blocks(2):
                    STT(tmp[:, 2, lo:hi], tmp[:, 2, lo:hi], 1.0, pap, ALU.add, ALU.mult, [("tmp", 2)], [("tmp", 2), ("ps", 2)])
                for (pap, lo, hi) in ps_blocks(3):
                    STT(tmp[:, 3, lo:hi], tmp[:, 3, lo:hi], 1.0, pap, ALU.add, ALU.mult, [("tmp", 3)], [("tmp", 3), ("ps", 3)])
                TT(hb[:, 32 + f, 0:T], tmp[:, 2, 0:T], tmp[:, 3, 0:T], ALU.add, [("tmp", 2), ("tmp", 3)], [("h", 32 + f)])
            P.tag = 'M4'
            Wo = w_out[l].rearrange("(kc p) n -> p kc n", p=128)
            for fp in range(NCH // 2):
                slot = wfill_op([(lambda s: wview(s, 0, 16, 256), Wo[:, :, fp * 256:(fp + 1) * 256])])
                wv = wview(slot, 0, 16, 256)
                for ff in range(2):
                    f = fp * 2 + ff
                    po = f % 3
                    MM(po, [(wv[:, k, ff * 128:(ff + 1) * 128], lambda lo, hi, k=k: hb[:, 32 + k, lo:hi]) for k in range(NCH)],
                       [("h", 32 + k) for k in range(NCH)] + [("w", slot)], [])
                    flush_stats()
                    out_evac(po, f, T, l, 1)
            flush_stats()
            postnorm(l, 3, T, 4.0 * EPS, False)

        xTv = xT.rearrange("(c p) t -> p c t", p=128)
        yTv = yT.rearrange("(c p) t -> p c t", p=128)
        ld_ops = []
        for ti, blks in enumerate(tiles):
            cur["blks"] = blks
            T = sum(b[2] for b in blks)
            xkeys = [("x", c) for c in range(NCH)]
            for (seq, t0, nn, c0) in blks:
                g0 = t0 if seq == 0 else p_len + t0
                misc_dma(xs[:, :, c0:c0 + nn], xTv[:, :, g0:g0 + nn], (), xkeys)
            for l in range(depth):
                ffn(l, 0, T)
                mixer(l, T, ti == 0)
                ffn(l, 1, T)
            for (seq, t0, nn, c0) in blks:
                g0 = t0 if seq == 0 else p_len + t0
                ld_ops.append(misc_dma(yTv[:, :, g0:g0 + nn], xs[:, :, c0:c0 + nn], xkeys, []))
        ld_ops.append(misc_dma(sout[:, :], state[:, :], ["state"], []))
        P.add("sp", lambda e: e.nop(), (), (), extra_deps=ld_ops)
        P.finalize()

        with nc.Block() as block:
            @block.tensor
            def _(e):
                P.emit("pe", e, sems)

            @block.scalar
            def _(e):
                P.emit("act", e, sems)

            @block.vector
            def _(e):
                P.emit("dve", e, sems)

            @block.gpsimd
            def _(e):
                P.emit("pool", e, sems)

            @block.sync
            def _(e):
                P.emit("sp", e, sems)
    return nc


def pack_pvec(norm_gains, conv_w, conv_b, b_rg_a, b_rg_x, lru_param, pool_scale):
    depth = norm_gains.shape[0]
    cols = []
    for l in range(depth):
        rows = [norm_gains[l].reshape(6 * 16, 128), conv_w[l].reshape(4 * 16, 128), conv_b[l].reshape(16, 128),
                b_rg_a[l].reshape(16, 128), b_rg_x[l].reshape(16, 128), lru_param[l].reshape(16, 128),
                pool_scale[l].reshape(8, 128)]
        cols.append(np.concatenate(rows, axis=0))
    return np.ascontiguousarray(np.concatenate(cols, axis=0).T.astype(np.float32))


def pack_state(state_conv, state_h, state_pool, b):
    depth = state_conv.shape[0]
    out = np.zeros((128, depth * NST), np.float32)
    for l in range(depth):
        o = l * NST
        out[:, o:o + 48] = state_conv[l, b].reshape(3, 16, 128).transpose(2, 1, 0).reshape(128, 48)
        out[:, o + 48:o + 64] = state_h[l, b].reshape(16, 128).T
        out[:, o + 64:o + 184] = state_pool[l, b].reshape(15, 8, 128).transpose(2, 1, 0).reshape(128, 120)
    return out


def unpack_state(so, depth):
    res = {}
    for l in range(depth):
        for seq in range(2):
            o = (l * 2 + seq) * NST
            conv = so[:, o:o + 48].reshape(128, 16, 3).transpose(2, 1, 0).reshape(3, 2048)
            h = so[:, o + 48:o + 64].T.reshape(2048)
            pool = so[:, o + 64:o + 184].reshape(128, 8, 15).transpose(2, 1, 0).reshape(15, 1024)
            res[(l, seq)] = (conv, h, pool)
    return res


def run(inputs, n_cores, depth, trace=False, annotate=False):
    x_prompt = np.asarray(inputs["x_prompt"])
    x_sample = np.asarray(inputs["x_sample"])
    p_len, s_len = x_prompt.shape[1], x_sample.shape[1]
    nc = build_nc(depth, p_len, s_len, annotate=annotate)
    f32 = lambda a: np.ascontiguousarray(np.asarray(a, dtype=np.float32))
    pvec = pack_pvec(*[np.asarray(inputs[k], dtype=np.float32) for k in
                       ("norm_gains", "conv_w", "conv_b", "b_rg_a", "b_rg_x", "lru_param", "pool_scale")])
    shared = {k: f32(inputs[k]) for k in ("w_ffn_in", "w_ffn_out", "w_in", "w_rg_a", "w_rg_x", "w_pool_mix",
                                          "w_br_rg", "w_br_pool", "w_out")}
    sc, sh, spl = (np.asarray(inputs[k], dtype=np.float32) for k in ("state_conv", "state_h", "state_pool"))
    in_maps = []
    for b in range(n_cores):
        m = dict(shared)
        m["xT"] = np.ascontiguousarray(np.concatenate([x_prompt[b].T, x_sample[b].T], axis=1).astype(np.float32))
        m["pvec"] = pvec
        m["sin"] = pack_state(sc, sh, spl, b)
        in_maps.append(m)
    res = run_bass_kernel_spmd(nc, in_maps, core_ids=list(range(n_cores)), **({"trace": True} if trace else {}))
    B = n_cores
    yp = np.zeros((B, p_len, D), np.float32)
    ys = np.zeros((B, s_len, D), np.float32)
    ncp = np.zeros((depth, B, 3, 2048), np.float32)
    nhp = np.zeros((depth, B, 2048), np.float32)
    npp = np.zeros((depth, B, 15, 1024), np.float32)
    ncs, nhs, nps = np.zeros_like(ncp), np.zeros_like(nhp), np.zeros_like(npp)
    for b in range(B):
        r = res.results[b]
        yT = r["yT"]
        yp[b] = yT[:, :p_len].T
        ys[b] = yT[:, p_len:].T
        st = unpack_state(r["sout"], depth)
        for l in range(depth):
            ncp[l, b], nhp[l, b], npp[l, b] = st[(l, 0)]
            ncs[l, b], nhs[l, b], nps[l, b] = st[(l, 1)]
    return (yp, ys, ncp, nhp, npp, ncs, nhs, nps), res


def kernel(**inputs):
    outs, _ = run(inputs, 8, 2)
    return outs
```
